# Optimizing a Trainium2 kernel written in Bass

```python
import jax, jax.numpy as jnp
from jax import lax
import numpy as np

D_MODEL = 2048
BATCH = 4
SEQ = 2048
DEPTH = 1
DEC_BATCH = 8
DEC_SEQ = 4096
PAST_LEN = 128

ATT_GROUPS = ((128, 1), (512, 4), (2048, 16))
ATT_HEADS_PER_GROUP = 4
ATT_HEAD_DIM = 128
ATT_HEADS = ATT_HEADS_PER_GROUP * len(ATT_GROUPS)
ATT_WIDTH = ATT_HEADS * ATT_HEAD_DIM
ATT_OUT = ATT_HEADS_PER_GROUP * ATT_HEAD_DIM
ROPE_THETA = 10000.0
RW_HEAD = 64
RW_WIDTH = D_MODEL // 2
RW_HEADS = RW_WIDTH // RW_HEAD
DECAY_LORA = 64
ICLR_LORA = 64
GATE_LORA = 160
RW_COLS = 3 * RW_WIDTH + 2 * DECAY_LORA + 2 * ICLR_LORA + GATE_LORA
N_IN = 3 * ATT_WIDTH + RW_COLS + 2 * D_MODEL
D_FF = ((8 * D_MODEL // 3 + 127) // 128) * 128
RMS_EPS = 1e-6
GN_EPS = 64e-5
NEG_INF = -1e30

kernel_name = 'hybrid_dilated_attn_rwkv7_encoder'


def rmsnorm(x, g):
    xf = x.astype(jnp.float32)
    y = xf * lax.rsqrt(jnp.mean(xf * xf, axis=-1, keepdims=True) + RMS_EPS)
    return (y * g.astype(jnp.float32)).astype(x.dtype)


def shift_prev(x):
    return jnp.pad(x, ((0, 0), (1, 0), (0, 0)))[:, :-1]


def shift_next(x):
    return jnp.pad(x, ((0, 0), (0, 1), (0, 0)))[:, 1:]


def rotary(x, pos):
    half = x.shape[-1] // 2
    inv_freq = 1.0 / (ROPE_THETA ** (jnp.arange(half, dtype=jnp.float32) / half))
    ang = pos.astype(jnp.float32)[:, None] * inv_freq[None, :]
    cos = jnp.cos(ang)[None, :, None, :]
    sin = jnp.sin(ang)[None, :, None, :]
    xf = x.astype(jnp.float32)
    x1, x2 = xf[..., :half], xf[..., half:]
    return jnp.concatenate([x1 * cos - x2 * sin, x2 * cos + x1 * sin], axis=-1).astype(x.dtype)


def dilated_window_attention(q, k, v, window, dilation):
    B, T, H, Dh = q.shape
    span = window // (2 * dilation)
    blk = span
    L = T // dilation
    nb = -(-L // blk)
    Lp = nb * blk

    def classes(t):
        t = t.reshape(B, L, dilation, H, Dh).transpose(0, 2, 1, 3, 4)
        return jnp.pad(t, ((0, 0), (0, 0), (0, Lp - L), (0, 0), (0, 0)))

    def band(t):
        t = jnp.pad(classes(t), ((0, 0), (0, 0), (blk, blk), (0, 0), (0, 0)))
        t = t.reshape(B, dilation, nb + 2, blk, H, Dh)
        return jnp.concatenate([t[:, :, :-2], t[:, :, 1:-1], t[:, :, 2:]], axis=3)

    qb = classes(q).reshape(B, dilation, nb, blk, H, Dh)
    kb = band(k)
    vb = band(v)
    qi = (jnp.arange(nb) * blk)[:, None] + jnp.arange(blk)[None, :]
    ki = (jnp.arange(nb) * blk)[:, None] + jnp.arange(-blk, 2 * blk)[None, :]
    mask = ((jnp.abs(qi[:, :, None] - ki[:, None, :]) <= span)
            & (ki[:, None, :] >= 0) & (ki[:, None, :] < L))
    s = jnp.einsum('bgnqhd,bgnkhd->bgnhqk', qb, kb, preferred_element_type=jnp.float32) * (Dh ** -0.5)
    s = jnp.where(mask[None, None, :, None], s, NEG_INF)
    m = jnp.max(s, axis=-1, keepdims=True)
    p = jnp.exp(s - m)
    l = jnp.sum(p, axis=-1, keepdims=True)
    o = jnp.einsum('bgnhqk,bgnkhd->bgnqhd', p / l, vb.astype(jnp.float32))
    lse = (m + jnp.log(l))[..., 0].transpose(0, 1, 2, 4, 3)
    o = o.reshape(B, dilation, Lp, H, Dh)[:, :, :L].transpose(0, 2, 1, 3, 4).reshape(B, T, H, Dh)
    lse = lse.reshape(B, dilation, Lp, H)[:, :, :L].transpose(0, 2, 1, 3).reshape(B, T, H)
    return o, lse


def dilated_attention_branch(q, k, v):
    B, T, _ = q.shape
    pos = jnp.arange(T)
    q = rotary(q.reshape(B, T, ATT_HEADS, ATT_HEAD_DIM), pos)
    k = rotary(k.reshape(B, T, ATT_HEADS, ATT_HEAD_DIM), pos)
    v = v.reshape(B, T, ATT_HEADS, ATT_HEAD_DIM)
    outs, lses = [], []
    for gi, (window, dilation) in enumerate(ATT_GROUPS):
        hs = slice(gi * ATT_HEADS_PER_GROUP, (gi + 1) * ATT_HEADS_PER_GROUP)
        o, lse = dilated_window_attention(q[:, :, hs], k[:, :, hs], v[:, :, hs], window, dilation)
        outs.append(o)
        lses.append(lse)
    wts = jax.nn.softmax(jnp.stack(lses), axis=0)[..., None]
    out = jnp.sum(wts * jnp.stack(outs), axis=0)
    return out.reshape(B, T, ATT_OUT).astype(q.dtype)


def rwkv7_bidir_branch(z, shift_mu, decay_base, decay_up, iclr_base, iclr_up, gate_up,
                       k_k, k_a, r_k, lnx_w, lnx_b):
    B, T, _ = z.shape
    f32 = jnp.float32
    z = z + shift_mu[0] * (shift_prev(z) - z) + shift_mu[1] * (shift_next(z) - z)
    c1, c2, c3 = RW_WIDTH, 2 * RW_WIDTH, 3 * RW_WIDTH
    c4 = c3 + 2 * DECAY_LORA
    c5 = c4 + 2 * ICLR_LORA
    r = z[..., :c1].astype(f32)
    k = z[..., c1:c2].astype(f32)
    v = z[..., c2:c3].astype(f32)
    wl = z[..., c3:c4].reshape(B, T, 2, DECAY_LORA).astype(f32)
    al = z[..., c4:c5].reshape(B, T, 2, ICLR_LORA).astype(f32)
    gl = z[..., c5:].astype(f32)
    w_raw = decay_base.astype(f32) + jnp.einsum('btzr,zrc->btzc', jnp.tanh(wl), decay_up.astype(f32))
    decay = jnp.exp(-jnp.exp(-jax.nn.softplus(-w_raw) - 0.5))
    a = jax.nn.sigmoid(iclr_base.astype(f32) + jnp.einsum('btzr,zrc->btzc', al, iclr_up.astype(f32)))
    g = jax.nn.sigmoid(gl) @ gate_up.astype(f32)

    def heads(t):
        return t.reshape(t.shape[:-1] + (RW_HEADS, RW_HEAD))

    kk = heads(k * k_k.astype(f32))
    kk = kk / jnp.maximum(jnp.sqrt(jnp.sum(kk * kk, axis=-1, keepdims=True)), 1e-12)
    k_dir = heads(k[:, :, None, :] * (1.0 + (a - 1.0) * k_a.astype(f32)))

    def shared(t):
        t = jnp.swapaxes(t, 0, 1)
        return jnp.stack([t, t[::-1]], axis=1)

    def per_dir(t):
        t = t.transpose(1, 2, 0, 3, 4)
        return jnp.stack([t[:, 0], t[::-1, 1]], axis=1)

    def step(S, inp):
        r_t, w_t, k_t, v_t, kk_t, a_t = inp
        sa = jnp.einsum('zbhij,zbhj->zbhi', S, -kk_t)
        S = (S * w_t[..., None, :] + sa[..., :, None] * (kk_t * a_t)[..., None, :]
             + v_t[..., :, None] * k_t[..., None, :])
        y = jnp.einsum('zbhij,zbhj->zbhi', S, r_t)
        return S, y

    S0 = jnp.zeros((2, B, RW_HEADS, RW_HEAD, RW_HEAD), f32)
    xs = (shared(heads(r)), per_dir(heads(decay)), per_dir(k_dir), shared(heads(v)), shared(kk),
          per_dir(heads(a)))
    _, ys = lax.scan(step, S0, xs)
    y = jnp.swapaxes(ys[:, 0] + ys[::-1, 1], 0, 1)
    mu = jnp.mean(y, axis=-1, keepdims=True)
    var = jnp.mean(jnp.square(y - mu), axis=-1, keepdims=True)
    y = ((y - mu) * lax.rsqrt(var + GN_EPS)).reshape(B, T, RW_WIDTH) * lnx_w.astype(f32) + lnx_b.astype(f32)
    bonus = jnp.sum(jnp.sum(heads(r)[:, :, None] * k_dir * r_k.astype(f32), axis=-1, keepdims=True), axis=2)
    y = y + (bonus * heads(v)).reshape(B, T, RW_WIDTH)
    return (y * g).astype(z.dtype)


def centred_dwconv3(u, w, b):
    return w[0] * shift_prev(u) + w[1] * u + w[2] * shift_next(u) + b


def encoder_layer(x, c, ln_mix_pre, ln_mix_post, ln_ffn_pre, ln_ffn_post, w_ada, b_ada, w_in,
                  shift_mu, decay_base, decay_up, iclr_base, iclr_up, gate_up, k_k, k_a, r_k,
                  lnx_w, lnx_b, w_att_branch, w_rwkv_branch, w_out,
                  w_ffn_gate, w_ffn_up, ffn_conv_w, ffn_conv_b, w_ffn_down):
    D = D_MODEL
    ada = (jax.nn.silu(c) @ w_ada + b_ada)[:, None, :]
    sh1, sc1, gt1, sh2, sc2, gt2 = jnp.split(ada, 6, axis=-1)
    h = rmsnorm(x, ln_mix_pre) * (1 + sc1) + sh1
    proj = h @ w_in
    o1, o2, o3 = ATT_WIDTH, 2 * ATT_WIDTH, 3 * ATT_WIDTH
    o4 = o3 + RW_COLS
    att = dilated_attention_branch(proj[..., :o1], proj[..., o1:o2], proj[..., o2:o3])
    rw = rwkv7_bidir_branch(proj[..., o3:o4], shift_mu, decay_base, decay_up, iclr_base, iclr_up,
                            gate_up, k_k, k_a, r_k, lnx_w, lnx_b)
    gate_att = jax.nn.sigmoid(proj[..., o4:o4 + D])
    gate_rw = jax.nn.sigmoid(proj[..., o4 + D:])
    mix = (gate_att * (att @ w_att_branch) + gate_rw * (rw @ w_rwkv_branch)) @ w_out
    x = x + gt1 * rmsnorm(mix, ln_mix_post)
    h = rmsnorm(x, ln_ffn_pre) * (1 + sc2) + sh2
    u = centred_dwconv3(h @ w_ffn_gate, ffn_conv_w, ffn_conv_b)
    f = (jax.nn.gelu(u, approximate=True) * (h @ w_ffn_up)) @ w_ffn_down
    return x + gt2 * rmsnorm(f, ln_ffn_post)


def setup_inputs(seed: int = 0) -> dict:
    key = jax.random.key(seed)
    ks = jax.random.split(key, 32)
    f32 = jnp.float32
    L, D = DEPTH, D_MODEL

    def nrm(k, shape, scale):
        return jax.random.normal(k, shape, f32) * scale

    def gain(k, shape, centre=1.0):
        return centre + 0.05 * jax.random.normal(k, shape, f32)

    return {
        'x_prompt': nrm(ks[0], (BATCH, SEQ, D), 1.0),
        'x_sample': nrm(ks[1], (DEC_BATCH, DEC_SEQ, D), 1.0),
        'c_prompt': nrm(ks[2], (BATCH, D), 1.0),
        'c_sample': nrm(ks[3], (DEC_BATCH, D), 1.0),
        'ln_mix_pre': gain(ks[4], (L, D)),
        'ln_mix_post': gain(ks[5], (L, D)),
        'ln_ffn_pre': gain(ks[6], (L, D)),
        'ln_ffn_post': gain(ks[7], (L, D)),
        'w_ada': nrm(ks[8], (L, D, 6 * D), 0.5 * D ** -0.5),
        'b_ada': nrm(ks[9], (L, 6 * D), 0.02),
        'w_in': nrm(ks[10], (L, D, N_IN), D ** -0.5),
        'shift_mu': jax.random.uniform(ks[11], (L, 2, RW_COLS), f32, 0.0, 0.5),
        'decay_base': jax.random.uniform(ks[12], (L, 2, RW_WIDTH), f32, -6.0, 0.0),
        'decay_up': nrm(ks[13], (L, 2, DECAY_LORA, RW_WIDTH), 0.5 * DECAY_LORA ** -0.5),
        'iclr_base': nrm(ks[14], (L, 2, RW_WIDTH), 0.3),
        'iclr_up': nrm(ks[15], (L, 2, ICLR_LORA, RW_WIDTH), 0.5 * ICLR_LORA ** -0.5),
        'gate_up': nrm(ks[16], (L, GATE_LORA, RW_WIDTH), GATE_LORA ** -0.5),
        'k_k': gain(ks[17], (L, RW_WIDTH), 0.85),
        'k_a': gain(ks[18], (L, RW_WIDTH), 1.0),
        'r_k': nrm(ks[19], (L, RW_HEADS, RW_HEAD), 0.1),
        'lnx_w': gain(ks[20], (L, RW_WIDTH)),
        'lnx_b': nrm(ks[21], (L, RW_WIDTH), 0.02),
        'w_att_branch': nrm(ks[22], (L, ATT_OUT, D), ATT_OUT ** -0.5),
        'w_rwkv_branch': nrm(ks[23], (L, RW_WIDTH, D), RW_WIDTH ** -0.5),
        'w_out': nrm(ks[24], (L, D, D), D ** -0.5),
        'w_ffn_gate': nrm(ks[25], (L, D, D_FF), D ** -0.5),
        'w_ffn_up': nrm(ks[26], (L, D, D_FF), D ** -0.5),
        'ffn_conv_w': nrm(ks[27], (L, 3, D_FF), 3 ** -0.5),
        'ffn_conv_b': nrm(ks[28], (L, D_FF), 0.02),
        'w_ffn_down': nrm(ks[29], (L, D_FF, D), D_FF ** -0.5),
    }


def reference(x_prompt, x_sample, c_prompt, c_sample, ln_mix_pre, ln_mix_post, ln_ffn_pre, ln_ffn_post,
              w_ada, b_ada, w_in, shift_mu, decay_base, decay_up, iclr_base, iclr_up, gate_up,
              k_k, k_a, r_k, lnx_w, lnx_b, w_att_branch, w_rwkv_branch, w_out,
              w_ffn_gate, w_ffn_up, ffn_conv_w, ffn_conv_b, w_ffn_down):
    def trunk(x, c):
        for l in range(DEPTH):
            x = encoder_layer(x, c, ln_mix_pre[l], ln_mix_post[l], ln_ffn_pre[l], ln_ffn_post[l],
                              w_ada[l], b_ada[l], w_in[l], shift_mu[l], decay_base[l], decay_up[l],
                              iclr_base[l], iclr_up[l], gate_up[l], k_k[l], k_a[l], r_k[l],
                              lnx_w[l], lnx_b[l], w_att_branch[l], w_rwkv_branch[l], w_out[l],
                              w_ffn_gate[l], w_ffn_up[l], ffn_conv_w[l], ffn_conv_b[l], w_ffn_down[l])
        return x

    y_prompt = trunk(x_prompt, c_prompt)
    y_sample = trunk(x_sample, c_sample)
    return (y_prompt, y_sample)
```

```python
import numpy as np
import concourse.bass as bass
import concourse.mybir as mybir
from concourse.bass_utils import run_bass_kernel_spmd

F32 = mybir.dt.float32
BF16 = mybir.dt.bfloat16
AF = mybir.ActivationFunctionType
ALU = mybir.AluOpType
AX = mybir.AxisListType


class Res:
    __slots__ = ("name", "lw", "rd", "excl")

    def __init__(self, name="", excl=False):
        self.name = name
        self.lw = None
        self.rd = []
        self.excl = excl


class Op:
    __slots__ = ("eng", "fn", "deps", "sig", "kind", "lane", "lane_ord", "sigkey", "sigval")


NLANES = {"sp": 8, "pool": 4, "act": 4}


class Prog:
    ENGS = ("pe", "act", "dve", "pool", "sp")

    def __init__(self, nc):
        self.nc = nc
        self.ops = []
        self.pending = {e: set() for e in self.ENGS}
        self.last_on = {}
        self.lane_last = {}
        self.lane_next = {q: 0 for q in NLANES}
        self.lane_cnt = {}

    def add(self, eng, fn, rd=(), wr=(), kind="c"):
        op = Op()
        i = len(self.ops)
        op.eng, op.fn, op.kind, op.sig = eng, fn, kind, False
        op.sigkey = None
        op.sigval = None
        if any(r.excl for r in rd):
            wr = list(wr) + [r for r in rd if r.excl]
            rd = [r for r in rd if not r.excl]
        deps = set()
        for r in rd:
            if r.lw is not None:
                deps.add(r.lw)
        for w in wr:
            if w.lw is not None:
                deps.add(w.lw)
            deps.update(w.rd)
        if self.pending[eng]:
            deps.update(self.pending[eng])
            self.pending[eng] = set()
        if kind == "dma":
            ln = self.lane_next[eng]
            self.lane_next[eng] = (ln + 1) % NLANES[eng]
            op.lane = (eng, ln)
            prev = self.lane_last.get(op.lane)
            if prev is not None:
                deps.add(prev)
            self.lane_last[op.lane] = i
            self.lane_cnt[op.lane] = self.lane_cnt.get(op.lane, 0) + 1
            op.lane_ord = self.lane_cnt[op.lane]
        else:
            self.last_on[eng] = i
        fdeps = []
        for d in deps:
            dop = self.ops[d]
            if dop.kind == "c" and dop.eng == "pe" and eng == "pe" and kind == "c":
                continue
            dop.sig = True
            fdeps.append(d)
        op.deps = sorted(fdeps)
        for r in rd:
            r.rd.append(i)
        for w in wr:
            w.lw = i
            w.rd = []
        self.ops.append(op)
        return i

    def barrier(self):
        s = set(self.last_on.values()) | set(self.lane_last.values())
        for e in self.ENGS:
            self.pending[e] |= s

    def emit(self):
        nc = self.nc
        eobj = {"pe": nc.tensor, "act": nc.scalar, "dve": nc.vector, "pool": nc.gpsimd, "sp": nc.sync}
        sems = {}
        for e in ("pe", "act", "dve", "pool"):
            sems[e] = nc.alloc_semaphore("sem_" + e)
        for q, n in NLANES.items():
            for l in range(n):
                sems[(q, l)] = nc.alloc_semaphore("lane_%s_%d" % (q, l))
        cnt = {e: 0 for e in self.ENGS}
        waited = {e: {} for e in self.ENGS}
        for op in self.ops:
            e = eobj[op.eng]
            wd = waited[op.eng]
            need = {}
            for d in op.deps:
                dop = self.ops[d]
                k, v = dop.sigkey, dop.sigval
                if wd.get(k, 0) < v and need.get(k, 0) < v:
                    need[k] = v
            for k, v in need.items():
                e.wait_ge(sems[k], v)
                wd[k] = v
            if op.fn is None:
                continue
            ins = op.fn()
            if op.kind == "dma":
                ins.then_inc(sems[op.lane], 16)
                op.sigkey, op.sigval = op.lane, 16 * op.lane_ord
            elif op.sig:
                cnt[op.eng] += 1
                ins.then_inc(sems[op.eng], 1)
                op.sigkey, op.sigval = op.eng, cnt[op.eng]
                if op.eng != "pe":
                    pass

    def dma(self, out, in_, rd=(), wr=(), q="sp", slow=False):
        nc = self.nc
        eobj = {"pool": nc.gpsimd, "act": nc.scalar, "sp": nc.sync}[q]
        if slow:
            return self.add(q, lambda: eobj.dma_start(out=out, in_=in_, allow_slow_non_contiguous=True), rd, wr, kind="dma")
        return self.add(q, lambda: eobj.dma_start(out=out, in_=in_), rd, wr, kind="dma")

    def mm(self, out, pairs, rd=(), wr=()):
        nc = self.nc
        n = len(pairs)

        def fn():
            ins = None
            for i, (l, r) in enumerate(pairs):
                ins = nc.tensor.matmul(out, l, r, start=(i == 0), stop=(i == n - 1))
            return ins

        return self.add("pe", fn, rd, wr)

    def tr(self, out, in_, ident, rd=(), wr=()):
        nc = self.nc
        return self.add("pe", lambda: nc.tensor.transpose(out, in_, ident), rd, wr)

    def act(self, out, in_, func, rd=(), wr=(), **kw):
        nc = self.nc
        return self.add("act", lambda: nc.scalar.activation(out=out, in_=in_, func=func, **kw), rd, wr)

    def V(self, eng, name, rd, wr, *a, **kw):
        nc = self.nc
        eo = nc.vector if eng == "dve" else nc.gpsimd
        f = getattr(eo, name)
        return self.add(eng, lambda: f(*a, **kw), rd, wr)


D = 2048
NIN = 12192
DFF = 5504
NFC = 43
RWC = 3488
ATT_GROUPS = ((128, 1), (512, 4), (2048, 16))
SEQ_T = (4096, 2048)


def _small_layout(TM):
    items = [("ln", 64), ("bada", 96), ("mu", 56), ("dbase", 16), ("ibase", 16), ("kk", 8), ("ka", 8),
             ("rk", 8), ("lnxw", 8), ("lnxb", 8), ("convw", 3 * NFC), ("convb", NFC),
             ("ident", 128), ("swap", 128), ("ones", 128), ("bones", 128), ("amask", 1024),
             ("rmask", 640), ("segmask", 512), ("eps", 2)]
    off = {}
    o = 0
    for k, n in items:
        off[k] = (o, n)
        o += n
    return off, o


class Ctx:
    pass


def build(seq_T=SEQ_T, do_rwkv=True, debug=False, upto=9):
    nc = bass.Bass("TRN2", target_bir_lowering=False)
    P = Prog(nc)
    NS = len(seq_T)
    TM = max(seq_T)
    SO, NSM = _small_layout(TM)

    def dram_in(name, shape, dt=F32):
        return nc.dram_tensor(name, list(shape), dt, kind="ExternalInput").ap()

    def dram_tmp(name, shape, dt):
        return nc.dram_tensor(name, list(shape), dt).ap()

    x_in = [dram_in("x%d" % s, [seq_T[s], D]) for s in range(NS)]
    y_out = [nc.dram_tensor("y%d" % s, [seq_T[s], D], F32, kind="ExternalOutput").ap() for s in range(NS)]
    cT_in = dram_in("cT", [128, 16 * NS])
    small_in = dram_in("small", [128, NSM])
    cs_in = dram_in("cossin", [128, 2 * TM])
    lnx_in = dram_in("lnxrow", [16, 128])
    w_ada = dram_in("w_ada", [D, 6 * D])
    w_in = dram_in("w_in", [D, NIN])
    w_att = dram_in("w_att", [512, D])
    w_rw = dram_in("w_rw", [1024, D])
    w_out = dram_in("w_out", [D, D])
    w_gate = dram_in("w_gate", [D, DFF])
    w_up = dram_in("w_up", [D, DFF])
    w_down = dram_in("w_down", [DFF, D])

    wb = {}
    for nm, src in (("ada", w_ada), ("in", w_in), ("att", w_att), ("rw", w_rw), ("out", w_out),
                    ("gate", w_gate), ("up", w_up), ("down", w_down)):
        wb[nm] = dram_tmp("wb_" + nm, src.shape, BF16)
    QPAD = TM + 2048
    qT_d = dram_tmp("qT_d", [12, 128, QPAD], BF16)
    kT_d = dram_tmp("kT_d", [12, 128, QPAD], BF16)
    v_d = dram_tmp("v_d", [TM, 1536], BF16)
    zT_d = dram_tmp("zT_d", [28, 128, TM + 2], F32)
    gT_d = dram_tmp("gT_d", [32, 128, TM], F32)
    attT_d = dram_tmp("attT_d", [4, 128, TM], BF16)
    rwT_d = dram_tmp("rwT_d", [8, 128, TM], BF16)
    x1_d = dram_tmp("x1_d", [TM, D], F32)
    h2T_d = dram_tmp("h2T_d", [16, 128, TM + 2], BF16)
    gvec_d = dram_tmp("gvec_d", [NS * 2, 16, 128], F32)
    yscr = dram_tmp("yscr", [8, TM // 512, 64, 8, 128], F32)
    dbg = {}

    PS8 = [nc.alloc_psum_tensor("psb%d" % i, [128, 512], F32) for i in range(8)]
    PSR8 = [Res("psb%d" % i, excl=True) for i in range(8)]
    PS, PSR = PS8[0:6], PSR8[0:6]
    PB = [PS8[6 + i][:, :].bitcast(BF16) for i in range(2)]
    PBR = [PSR8[6 + i] for i in range(2)]

    def sb(name, shape, dt):
        return nc.alloc_sbuf_tensor(name, list(shape), dt)

    small = sb("small_sb", [128, NSM], F32)
    Rsmall = Res("small")
    P.dma(small[:, :], small_in[:, :], wr=[Rsmall])

    def sm(key, a=0, b=None):
        o, n = SO[key]
        if b is None:
            b = n
        return small[:, o + a:o + b]

    cbf = sb("cbf", [128, 128 * 4 + 1024 + 640], BF16)
    Rcbf = Res("cbf")
    o_id = SO["ident"][0]
    P.V("dve", "tensor_copy", [Rsmall], [Rcbf], cbf[:, 0:512 + 1024 + 640], small[:, o_id:o_id + 512 + 1024 + 640])
    ident = cbf[:, 0:128]
    swapm = cbf[:, 128:256]
    ones = cbf[:, 256:384]
    bones = cbf[:, 384:512]
    amask = cbf[:, 512:1536]
    rmask = cbf[:, 1536:1536 + 640]
    identf = sm("ident")
    upbf = sb("upbf", [128, 4096], BF16)
    Rup = Res("upbf")
    ups_in = dram_in("ups", [128, 4096])
    with nc.sbuf_tensor("upsf", [128, 4096], F32) as upsf:
        Rupsf = Res("upsf")
        P.dma(upsf[:, :], ups_in[:, :], wr=[Rupsf])
        P.V("dve", "tensor_copy", [Rupsf], [Rup], upbf[:, :], upsf[:, :])
        P.barrier()

    Rw = {}
    for nm, src in (("ada", w_ada), ("in", w_in), ("att", w_att), ("rw", w_rw), ("out", w_out),
                    ("gate", w_gate), ("up", w_up), ("down", w_down)):
        Rw[nm] = Res("w" + nm)
        rows = src.shape[0]
        for r0 in range(0, rows, 128):
            P.dma(wb[nm][r0:r0 + 128, :], src[r0:r0 + 128, :], wr=[], q="pool")
    P.barrier()

    zt = sb("zerot", [128, 512], F32)
    Rzt = Res("zt")
    P.V("pool", "memset", [], [Rzt], zt[:, :], 0.0)
    ztb = zt[:, :].bitcast(BF16)
    for hh in range(12):
        for t0 in range(0, QPAD, 1024):
            n = min(1024, QPAD - t0)
            P.dma(kT_d[hh, :, t0:t0 + n], ztb[:, 0:n], rd=[Rzt])
    for ch in range(28):
        P.dma(zT_d[ch, :, 0:1], zt[:, 0:1], rd=[Rzt], slow=True)
    for ch in range(16):
        P.dma(h2T_d[ch, :, 0:1], ztb[:, 0:1], rd=[Rzt], slow=True)
    if not do_rwkv:
        for ch in range(8):
            for t0 in range(0, TM, 1024):
                P.dma(rwT_d[ch, :, t0:min(TM, t0 + 1024)], ztb[:, 0:min(1024, TM - t0)], rd=[Rzt])
    P.barrier()

    mod = sb("mod", [128, NS, 6, 16], F32)
    Rmod = Res("mod")
    with nc.sbuf_tensor("adaw", [128, 2, 16, 512], BF16) as adaw, \
            nc.sbuf_tensor("adatmp", [128, 112 * NS + 128], F32) as adatmp, \
            nc.sbuf_tensor("scb", [128, 16 * NS], BF16) as scb:
        Radaw = [Res("adaw0"), Res("adaw1")]
        Rtmp = Res("adatmp")
        Rscb = Res("scb")
        ctile = adatmp[:, 96 * NS:96 * NS + 16 * NS]
        P.dma(ctile, cT_in[:, :], wr=[Rtmp])
        P.act(scb[:, :], ctile, AF.Silu, rd=[Rtmp], wr=[Rscb])
        for blk in range(24):
            b = blk % 2
            P.dma(adaw[:, b, :, :], wb["ada"][:, blk * 512:(blk + 1) * 512].rearrange("(kc p) n -> p kc n", p=128),
                  wr=[Radaw[b]])
            for j in range(4):
                n = blk * 4 + j
                P.mm(PS[0][:, n * NS:(n + 1) * NS],
                     [(adaw[:, b, kc, j * 128:(j + 1) * 128], scb[:, kc * NS:(kc + 1) * NS]) for kc in range(16)],
                     rd=[Radaw[b], Rscb], wr=[PSR[0]])
        ada = adatmp[:, 0:96 * NS]
        P.V("dve", "tensor_tensor", [PSR[0], Rsmall], [Rtmp],
            ada.rearrange("p (n s) -> p n s", s=NS), PS[0][:, 0:96 * NS].rearrange("p (n s) -> p n s", s=NS),
            sm("bada").unsqueeze(2).to_broadcast([128, 96, NS]), ALU.add)
        adav = ada.rearrange("p (n s) -> p n s", s=NS)
        ln = sm("ln")
        for s in range(NS):
            for half in range(2):
                sh = adav[:, (3 * half) * 16:(3 * half + 1) * 16, s]
                sc = adav[:, (3 * half + 1) * 16:(3 * half + 2) * 16, s]
                gt = adav[:, (3 * half + 2) * 16:(3 * half + 3) * 16, s]
                lpre = ln[:, (2 * half) * 16:(2 * half + 1) * 16]
                lpost = ln[:, (2 * half + 1) * 16:(2 * half + 2) * 16]
                P.V("dve", "tensor_tensor", [Rtmp, Rsmall], [Rmod], mod[:, s, 3 * half, :], sc, lpre, ALU.mult)
                P.V("dve", "tensor_tensor", [Rmod, Rsmall], [Rmod], mod[:, s, 3 * half, :], mod[:, s, 3 * half, :], lpre, ALU.add)
                P.V("dve", "tensor_copy", [Rtmp], [Rmod], mod[:, s, 3 * half + 1, :], sh)
                P.V("dve", "tensor_tensor", [Rtmp, Rsmall], [Rmod], mod[:, s, 3 * half + 2, :], gt, lpost, ALU.mult)
                P.tr(PS[1][0:16, 0:128], mod[:, s, 3 * half + 2, :], identf, rd=[Rmod, Rsmall], wr=[PSR[1]])
                P.V("dve", "tensor_copy", [PSR[1]], [Rtmp], adatmp[0:16, 112 * NS:112 * NS + 128], PS[1][0:16, 0:128])
                P.dma(gvec_d[s * 2 + half, :, :], adatmp[0:16, 112 * NS:112 * NS + 128], rd=[Rtmp])
    P.barrier()
    C = Ctx()
    C.__dict__.update(locals())
    for s in range(NS):
        if upto >= 2:
            emit_sequence(C, s)
    P.barrier()
    P.add("sp", None)
    P.emit()
    return nc


def _sections():
    secs = [("q", 0, 1536), ("k", 1536, 1536), ("v", 3072, 1536), ("z", 4608, RWC), ("g", 4608 + RWC, 4096)]
    blocks = []
    for kind, c0, n in secs:
        for b0 in range(0, n, 512):
            blocks.append((kind, c0 + b0, min(512, n - b0), b0 // 128))
    return blocks


def phase_A(C, s):
    nc, P = C.nc, C.P
    T = C.seq_T[s]
    TS = min(1024, T)
    NST = TS // 128
    NTT = TS // 512
    mod, sm = C.mod, C.sm
    PS, PSR, PB, PBR = C.PS, C.PSR, C.PB, C.PBR
    blocks = _sections()
    with nc.sbuf_tensor("hT_s%d" % s, [128, 16, TS], BF16) as hT, \
            nc.sbuf_tensor("wblk_s%d" % s, [128, 2, 16, 512], BF16) as wblk, \
            nc.sbuf_tensor("xt_s%d" % s, [128, 2, D], F32) as xt, \
            nc.sbuf_tensor("xnb_s%d" % s, [128, 2, D], BF16) as xnb, \
            nc.sbuf_tensor("stat_s%d" % s, [128, 2, 4], F32) as stat, \
            nc.sbuf_tensor("cs_s%d" % s, [128, 2, TS], F32) as cs, \
            nc.sbuf_tensor("stg_s%d" % s, [128, 3, 512], F32) as stg, \
            nc.sbuf_tensor("stgb_s%d" % s, [128, 3, 512], BF16) as stgb, \
            nc.sbuf_tensor("xb_s%d" % s, [128, 2, 512], BF16) as xb, \
            nc.sbuf_tensor("tmpA_s%d" % s, [128, 2, 512], F32) as tmpA, \
            nc.sbuf_tensor("tmpB_s%d" % s, [128, 2, 512], F32) as tmpB:
        RhT = [Res("hT%d" % i) for i in range(NST)]
        Rwblk = [Res("wblk0"), Res("wblk1")]
        Rxt = [Res("xt0"), Res("xt1")]
        Rxnb = [Res("xnb0"), Res("xnb1")]
        Rstat = [Res("stat0"), Res("stat1")]
        Rcs = Res("cs")
        Rstg = [Res("stg%d" % i) for i in range(3)]
        Rstgb = [Res("stgb%d" % i) for i in range(3)]
        Rxb = [Res("xb0"), Res("xb1")]
        RtA = [Res("tA0"), Res("tA1")]
        RtB = [Res("tB0"), Res("tB1")]
        cnt = {"ps": 0, "ps2": 0, "stg": 0, "stgb": 0, "xb": 0, "tm": 0}

        def rot(k, n):
            v = cnt[k] % n
            cnt[k] += 1
            return v

        for t_s in range(0, T, TS):
            P.dma(cs[:, 0, :], C.cs_in[:, t_s:t_s + TS], wr=[Rcs])
            P.dma(cs[:, 1, :], C.cs_in[:, C.TM + t_s:C.TM + t_s + TS], wr=[Rcs])
            P.dma(xt[:, 0, :], C.x_in[s][t_s:t_s + 128, :], wr=[Rxt[0]])
            for st in range(NST):
                b = st % 2
                if st + 1 < NST:
                    P.dma(xt[:, 1 - b, :], C.x_in[s][t_s + (st + 1) * 128:t_s + (st + 2) * 128, :], wr=[Rxt[1 - b]])
                P.V("dve", "memset", [], [Rstat[b]], stat[:, b, 0:1], 0.0)
                P.act(xnb[:, b, :], xt[:, b, :], AF.Square, rd=[Rxt[b], Rstat[b]], wr=[Rxnb[b], Rstat[b]],
                      accum_out=stat[:, b, 0:1])
                P.act(stat[:, b, 1:2], stat[:, b, 0:1], AF.Sqrt, rd=[Rstat[b], C.Rsmall], wr=[Rstat[b]],
                      scale=1.0 / D, bias=sm("eps", 0, 1))
                P.V("dve", "reciprocal", [Rstat[b]], [Rstat[b]], stat[:, b, 2:3], stat[:, b, 1:2])
                P.V("dve", "tensor_scalar", [Rxt[b], Rstat[b]], [Rxnb[b]], xnb[:, b, :], xt[:, b, :],
                    stat[:, b, 2:3], None, ALU.mult)
                for kc in range(16):
                    P.tr(PB[kc // 8][:, (kc % 8) * 128:(kc % 8 + 1) * 128], xnb[:, b, kc * 128:(kc + 1) * 128],
                         C.ident, rd=[Rxnb[b], C.Rcbf], wr=[PBR[kc // 8]])
                for kc in range(16):
                    P.act(hT[:, kc, st * 128:(st + 1) * 128], PB[kc // 8][:, (kc % 8) * 128:(kc % 8 + 1) * 128],
                          AF.Identity, rd=[PBR[kc // 8], C.Rmod], wr=[RhT[st]],
                          scale=mod[:, s, 0, kc:kc + 1], bias=mod[:, s, 1, kc:kc + 1])

            def load_w(bi):
                kind, c0, ncol, i0 = blocks[bi]
                bb = bi % 2
                P.dma(wblk[:, bb, :, 0:ncol], C.wb["in"][:, c0:c0 + ncol].rearrange("(kc p) n -> p kc n", p=128),
                      wr=[Rwblk[bb]])

            import os
            kinds = os.environ.get("KDBG", "q,k,v,z,g").split(",")
            blocks = [b for b in _sections() if b[0] in kinds]
            if not blocks:
                continue
            load_w(0)
            for bi, (kind, c0, ncol, i0) in enumerate(blocks):
                bb = bi % 2
                if bi + 1 < len(blocks):
                    load_w(bi + 1)
                if kind == "v":
                    for st in range(NST):
                        k = rot("ps", 3)
                        P.mm(PS[k][:, 0:512], [(hT[:, kc, st * 128:(st + 1) * 128], wblk[:, bb, kc, 0:512]) for kc in range(16)],
                             rd=[RhT[st], Rwblk[bb]], wr=[PSR[k]])
                        i = rot("stgb", 3)
                        P.act(stgb[:, i, :], PS[k][:, 0:512], AF.Copy, rd=[PSR[k]], wr=[Rstgb[i]])
                        tg = t_s + st * 128
                        P.dma(C.v_d[tg:tg + 128, i0 * 128:i0 * 128 + 512], stgb[:, i, :], rd=[Rstgb[i]])
                    continue
                for j in range((ncol + 127) // 128):
                    m = min(128, ncol - j * 128)
                    ch = i0 + j
                    for tt in range(NTT):
                        tg = t_s + tt * 512
                        k = rot("ps", 3)
                        P.mm(PS[k][0:m, 0:512],
                             [(wblk[:, bb, kc, j * 128:j * 128 + m], hT[:, kc, tt * 512:(tt + 1) * 512]) for kc in range(16)],
                             rd=[RhT[4 * tt + q] for q in range(4)] + [Rwblk[bb]], wr=[PSR[k]])
                        if kind == "z":
                            i = rot("stg", 3)
                            P.act(stg[0:m, i, :], PS[k][0:m, 0:512], AF.Copy, rd=[PSR[k]], wr=[Rstg[i]])
                            P.dma(C.zT_d[ch, 0:m, 1 + tg:1 + tg + 512], stg[0:m, i, :], rd=[Rstg[i]])
                        elif kind == "g":
                            i = rot("stg", 3)
                            P.act(stg[:, i, :], PS[k][:, 0:512], AF.Sigmoid, rd=[PSR[k]], wr=[Rstg[i]])
                            P.dma(C.gT_d[ch, :, tg:tg + 512], stg[:, i, :], rd=[Rstg[i]])
                        else:
                            hd = ch
                            if hd >= int(os.environ.get("QH", "99")):
                                continue
                            dil = ATT_GROUPS[hd // 4][1]
                            Lp = T // dil + 128
                            xi = rot("xb", 2)
                            P.act(xb[:, xi, :], PS[k][:, 0:512], AF.Copy, rd=[PSR[k]], wr=[Rxb[xi]])
                            k2 = 3 + rot("ps2", 2)
                            P.mm(PS[k2][:, 0:512], [(C.swapm, xb[:, xi, :])], rd=[Rxb[xi], C.Rcbf], wr=[PSR[k2]])
                            ti = rot("tm", 2)
                            P.V("dve", "tensor_tensor", [PSR[k], Rcs], [RtA[ti]], tmpA[:, ti, :], PS[k][:, 0:512],
                                cs[:, 0, tt * 512:(tt + 1) * 512], ALU.mult)
                            P.V("dve", "tensor_tensor", [PSR[k2], Rcs], [RtB[ti]], tmpB[:, ti, :], PS[k2][:, 0:512],
                                cs[:, 1, tt * 512:(tt + 1) * 512], ALU.mult)
                            i = rot("stgb", 3)
                            P.V(os.environ.get("ROTENG", "pool"), "tensor_tensor", [RtA[ti], RtB[ti]], [Rstgb[i]],
                                stgb[:, i, :].rearrange("p (r m) -> p m r", r=dil),
                                tmpA[:, ti, :].rearrange("p (m r) -> p m r", r=dil),
                                tmpB[:, ti, :].rearrange("p (m r) -> p m r", r=dil), ALU.add)
                            dst = (C.qT_d if kind == "q" else C.kT_d)[hd, :, 0:dil * Lp].rearrange("p (r m) -> p r m", r=dil)
                            mo = 64 + tg // dil
                            if not os.environ.get("NOQDMA"):
                                P.dma(dst[:, :, mo:mo + 512 // dil], stgb[:, i, :].rearrange("p (r m) -> p r m", r=dil),
                                      rd=[Rstgb[i]])


def phase_attn(C, s):
    nc, P = C.nc, C.P
    T = C.seq_T[s]
    PS, PSR = C.PS, C.PSR
    TQ = T + 2048
    with nc.sbuf_tensor("Qc_s%d" % s, [128, TQ], BF16) as Qc, \
            nc.sbuf_tensor("Kc_s%d" % s, [128, TQ], BF16) as Kc, \
            nc.sbuf_tensor("Vb_s%d" % s, [128, 2, T // 128 + 1, 128], BF16) as Vb, \
            nc.sbuf_tensor("accden_s%d" % s, [128, 2, T], F32) as accden, \
            nc.sbuf_tensor("E_s%d" % s, [128, 2, 256], BF16) as E, \
            nc.sbuf_tensor("attb_s%d" % s, [128, T], BF16) as attb:
        RQ, RK, RAD, Ratt = Res("Qc"), Res("Kc"), Res("accden"), Res("attb")
        RV = [Res("Vb0"), Res("Vb1")]
        RE = [Res("E0"), Res("E1")]
        n_e = 0
        n_v = 0
        scale = 128.0 ** -0.5
        for h in range(4):
            P.V("pool", "memset", [], [RAD], accden[:, :, :], 0.0)
            for g in range(3):
                dil = ATT_GROUPS[g][1]
                L = T // dil
                Lp = L + 128
                nb = L // 128
                hd = 4 * g + h
                P.dma(Qc[:, 0:dil * Lp], C.qT_d[hd, :, 0:dil * Lp], wr=[RQ])
                P.dma(Kc[:, 0:dil * Lp], C.kT_d[hd, :, 0:dil * Lp], wr=[RK])
                cols = slice(hd * 128, (hd + 1) * 128)
                for r in range(dil):
                    vb = n_v % 2
                    n_v += 1
                    if nb > 1:
                        a0 = r + dil * 64
                        src = C.v_d[a0:a0 + dil * 128 * (nb - 1), cols].rearrange("(c p r) d -> p c r d", p=128, r=dil)[:, :, 0, :]
                        P.dma(Vb[:, vb, 1:nb, :], src, wr=[RV[vb]])
                    P.V("pool", "memset", [], [RV[vb]], Vb[0:64, vb, 0, :], 0.0)
                    P.V("pool", "memset", [], [RV[vb]], Vb[64:128, vb, nb, :], 0.0)
                    src0 = C.v_d[r:r + dil * 64, cols].rearrange("(p r) d -> p r d", r=dil)[:, 0, :]
                    P.dma(Vb[64:128, vb, 0, :], src0, wr=[RV[vb]])
                    a1 = r + dil * ((nb - 1) * 128 + 64)
                    src1 = C.v_d[a1 - r:a1 - r + dil * 64, cols].rearrange("(p r) d -> p r d", r=dil)[:, r, :]
                    P.dma(Vb[0:64, vb, nb, :], src1, wr=[RV[vb]])
                    import os
                    adbg = int(os.environ.get("ATTDBG", "9"))
                    for qb in range(nb):
                        if adbg < 2:
                            break
                        q0 = qb * 128
                        ks0 = slice(0, 128)
                        ks1 = slice(0, 128)
                        mi = (1 if qb == 0 else 0) + (2 if qb == nb - 1 else 0)
                        kS = n_e % 2
                        kO = 2 + n_e % 2
                        e = n_e % 2
                        n_e += 1
                        qv = Qc[:, r * Lp + 64 + q0:r * Lp + 64 + q0 + 128]
                        P.mm(PS[kS][:, 0:128], [(Kc[:, r * Lp + q0:r * Lp + q0 + 128], qv)], rd=[RQ, RK], wr=[PSR[kS]])
                        P.mm(PS[kS][:, 128:256], [(Kc[:, r * Lp + q0 + 128:r * Lp + q0 + 256], qv)], rd=[RQ, RK], wr=[PSR[kS]])
                        P.act(E[:, e, :], PS[kS][:, 0:256], AF.Exp, rd=[PSR[kS]], wr=[RE[e]], scale=scale)
                        P.V("pool", "tensor_tensor", [RE[e], C.Rcbf], [RE[e]], E[:, e, :], E[:, e, :], C.amask[:, mi * 256:(mi + 1) * 256], ALU.mult)
                        if adbg < 3:
                            continue
                        P.mm(PS[kO][:, 0:128], [(Vb[ks0, vb, qb, :], E[ks0, e, 0:128]), (Vb[ks1, vb, qb + 1, :], E[ks1, e, 128:256])],
                             rd=[RV[vb], RE[e]], wr=[PSR[kO]])
                        P.mm(PS[kO][:, 128:256], [(C.ones[ks0, :], E[ks0, e, 0:128]), (C.ones[ks1, :], E[ks1, e, 128:256])],
                             rd=[C.Rcbf, RE[e]], wr=[PSR[kO]])
                        if adbg < 4:
                            continue
                        adv = accden[:, :, 0:T].rearrange("p a (m r) -> p a m r", r=dil)[:, :, q0:q0 + 128, r]
                        P.V("dve", "tensor_tensor", [RAD, PSR[kO]], [RAD], adv, adv,
                            PS[kO][:, 0:256].rearrange("p (a m) -> p a m", a=2), ALU.add)
            P.V("dve", "reciprocal", [RAD], [RAD], accden[:, 1, :], accden[:, 1, :])
            P.V("dve", "tensor_tensor", [RAD], [Ratt], attb[:, :], accden[:, 0, :], accden[:, 1, :], ALU.mult)
            P.dma(C.attT_d[h, :, 0:T], attb[:, :], rd=[Ratt])


def _norm_rows(C, P, src_ap, nrows, statt, Rst, col, Rsrc, junk_ap, Rjunk):
    P.V("dve", "memset", [], [Rst], statt[0:nrows, col:col + 1], 0.0)
    P.act(junk_ap, src_ap, AF.Square, rd=[Rsrc, Rst], wr=[Rjunk, Rst], accum_out=statt[0:nrows, col:col + 1])
    P.act(statt[0:nrows, col + 1:col + 2], statt[0:nrows, col:col + 1], AF.Sqrt, rd=[Rst, C.Rsmall], wr=[Rst],
          scale=1.0 / D, bias=C.sm("eps", 0, 1)[0:nrows, :])
    P.V("dve", "reciprocal", [Rst], [Rst], statt[0:nrows, col + 2:col + 3], statt[0:nrows, col + 1:col + 2])


def phase_C(C, s):
    nc, P = C.nc, C.P
    T = C.seq_T[s]
    PS, PSR, PB, PBR = C.PS, C.PSR, C.PB, C.PBR
    mod = C.mod
    with nc.sbuf_tensor("watt_s%d" % s, [128, 4, D], BF16) as watt, \
            nc.sbuf_tensor("wrw_s%d" % s, [128, 8, D], BF16) as wrw, \
            nc.sbuf_tensor("wout_s%d" % s, [128, 16, D], BF16) as wout, \
            nc.sbuf_tensor("attt_s%d" % s, [128, 4, 512], BF16) as attt, \
            nc.sbuf_tensor("rwt_s%d" % s, [128, 8, 512], BF16) as rwt, \
            nc.sbuf_tensor("gts_s%d" % s, [128, 1, 2, 512], F32) as gts, \
            nc.sbuf_tensor("mT_s%d" % s, [128, 16, 512], BF16) as mT, \
            nc.sbuf_tensor("tAB_s%d" % s, [128, 1, 2, 512], F32) as tAB, \
            nc.sbuf_tensor("g1b_s%d" % s, [128, D], F32) as g1b, \
            nc.sbuf_tensor("xc_s%d" % s, [128, D], F32) as xc, \
            nc.sbuf_tensor("x1t_s%d" % s, [128, D], F32) as x1t, \
            nc.sbuf_tensor("xnc_s%d" % s, [128, D], BF16) as xnc, \
            nc.sbuf_tensor("h2s_s%d" % s, [128, 16, 128], BF16) as h2s, \
            nc.sbuf_tensor("stc_s%d" % s, [128, 8], F32) as stc:
        Rw3 = Res("wC")
        Ratt, Rrw, RmT, Rg1, Rxc, Rx1, Rxn, Rh2, Rst = (Res("attt"), Res("rwt"), Res("mT"), Res("g1b"), Res("xc"),
                                                        Res("x1t"), Res("xnc"), Res("h2s"), Res("stc"))
        Rg = [Res("gts0"), Res("gts1")]
        RtAB = [Res("tAB0"), Res("tAB1")]
        P.dma(watt[:, :, :], C.wb["att"][:, :].rearrange("(kc p) n -> p kc n", p=128), wr=[Rw3])
        P.dma(wrw[:, :, :], C.wb["rw"][:, :].rearrange("(kc p) n -> p kc n", p=128), wr=[Rw3])
        for kc in range(16):
            P.dma(wout[:, kc, :], C.wb["out"][kc * 128:(kc + 1) * 128, :], wr=[Rw3])
        P.dma(g1b[:, :], C.gvec_d[s * 2 + 0, :, :].rearrange("a b -> (a b)").partition_broadcast(128), wr=[Rg1])
        for t0 in range(0, T, 512):
            P.dma(attt[:, :, :], C.attT_d[:, :, t0:t0 + 512].rearrange("c p t -> p c t"), wr=[Ratt])
            P.dma(rwt[:, :, :], C.rwT_d[:, :, t0:t0 + 512].rearrange("c p t -> p c t"), wr=[Rrw])
            for n in range(16):
                b = 0
                P.dma(gts[:, b, 0, :], C.gT_d[n, :, t0:t0 + 512], wr=[Rg[b]])
                P.dma(gts[:, b, 1, :], C.gT_d[16 + n, :, t0:t0 + 512], wr=[Rg[b]])
                P.mm(PS[0][:, 0:512], [(watt[:, kc, n * 128:(n + 1) * 128], attt[:, kc, :]) for kc in range(4)],
                     rd=[Rw3, Ratt], wr=[PSR[0]])
                P.mm(PS[1][:, 0:512], [(wrw[:, kc, n * 128:(n + 1) * 128], rwt[:, kc, :]) for kc in range(8)],
                     rd=[Rw3, Rrw], wr=[PSR[1]])
                P.V("dve", "tensor_tensor", [PSR[0], Rg[b]], [RtAB[b]], tAB[:, b, 0, :], PS[0][:, 0:512], gts[:, b, 0, :], ALU.mult)
                P.V("dve", "tensor_tensor", [PSR[1], Rg[b]], [RtAB[b]], tAB[:, b, 1, :], PS[1][:, 0:512], gts[:, b, 1, :], ALU.mult)
                P.V("pool", "tensor_tensor", [RtAB[b]], [RmT], mT[:, n, :], tAB[:, b, 0, :], tAB[:, b, 1, :], ALU.add)
            for st in range(4):
                tg = t0 + st * 128
                P.dma(xc[:, :], C.x_in[s][tg:tg + 128, :], wr=[Rxc])
                for nb in range(4):
                    P.mm(PS[2 + nb][:, 0:512], [(mT[:, kc, st * 128:(st + 1) * 128], wout[:, kc, nb * 512:(nb + 1) * 512]) for kc in range(16)],
                         rd=[RmT, Rw3], wr=[PSR[2 + nb]])
                P.V("dve", "memset", [], [Rst], stc[:, 0:4], 0.0)
                for nb in range(4):
                    P.act(x1t[:, nb * 512:(nb + 1) * 512], PS[2 + nb][:, 0:512], AF.Square, rd=[PSR[2 + nb], Rst], wr=[Rx1, Rst],
                          accum_out=stc[:, nb:nb + 1])
                P.V("dve", "tensor_reduce", [Rst], [Rst], stc[:, 4:5], stc[:, 0:4], AX.X, ALU.add)
                P.act(stc[:, 5:6], stc[:, 4:5], AF.Sqrt, rd=[Rst, C.Rsmall], wr=[Rst], scale=1.0 / D, bias=C.sm("eps", 0, 1))
                P.V("dve", "reciprocal", [Rst], [Rst], stc[:, 6:7], stc[:, 5:6])
                for nb in range(4):
                    sl = slice(nb * 512, (nb + 1) * 512)
                    P.V("dve", "scalar_tensor_tensor", [PSR[2 + nb], Rst, Rg1, Rx1], [Rx1], x1t[:, sl], PS[2 + nb][:, 0:512],
                        stc[:, 6:7], g1b[:, sl], ALU.mult, ALU.mult)
                P.V("pool", "tensor_tensor", [Rx1, Rxc], [Rx1], x1t[:, :], x1t[:, :], xc[:, :], ALU.add)
                P.dma(C.x1_d[tg:tg + 128, :], x1t[:, :], rd=[Rx1])
                P.V("dve", "memset", [], [Rst], stc[:, 7:8], 0.0)
                P.act(xnc[:, :], x1t[:, :], AF.Square, rd=[Rx1, Rst], wr=[Rxn, Rst], accum_out=stc[:, 7:8])
                P.act(stc[:, 5:6], stc[:, 7:8], AF.Sqrt, rd=[Rst, C.Rsmall], wr=[Rst], scale=1.0 / D, bias=C.sm("eps", 0, 1))
                P.V("dve", "reciprocal", [Rst], [Rst], stc[:, 6:7], stc[:, 5:6])
                P.V("dve", "tensor_scalar", [Rx1, Rst], [Rxn], xnc[:, :], x1t[:, :], stc[:, 6:7], None, ALU.mult)
                for kc in range(16):
                    P.tr(PB[kc // 8][:, (kc % 8) * 128:(kc % 8 + 1) * 128], xnc[:, kc * 128:(kc + 1) * 128], C.ident,
                         rd=[Rxn, C.Rcbf], wr=[PBR[kc // 8]])
                for kc in range(16):
                    P.act(h2s[:, kc, :], PB[kc // 8][:, (kc % 8) * 128:(kc % 8 + 1) * 128], AF.Identity,
                          rd=[PBR[kc // 8], C.Rmod], wr=[Rh2], scale=mod[:, s, 3, kc:kc + 1], bias=mod[:, s, 4, kc:kc + 1])
                P.dma(C.h2T_d[:, :, 1 + tg:1 + tg + 128].rearrange("c p t -> p c t"), h2s[:, :, :], rd=[Rh2])
        P.dma(C.h2T_d[:, :, 1 + T:2 + T].rearrange("c p t -> p c t"), C.ztb[:, 0:16].rearrange("p (c t) -> p c t", t=1), rd=[C.Rzt], slow=True)


def phase_D(C, s):
    nc, P = C.nc, C.P
    T = C.seq_T[s]
    PS, PSR = C.PS8, C.PSR8
    sm = C.sm
    starts = list(range(0, T - 510, 510)) + [T - 510]
    with nc.sbuf_tensor("h2t_s%d" % s, [128, 2, 16, 512], BF16) as h2t, \
            nc.sbuf_tensor("wgu_s%d" % s, [128, 2, 2, 16, 128], BF16) as wgu, \
            nc.sbuf_tensor("aT_s%d" % s, [128, NFC, 512], BF16) as aT, \
            nc.sbuf_tensor("wd_s%d" % s, [128, 4, D], BF16) as wd, \
            nc.sbuf_tensor("cv_s%d" % s, [128, 2, 2, 512], F32) as cv, \
            nc.sbuf_tensor("g2b_s%d" % s, [128, D], F32) as g2b, \
            nc.sbuf_tensor("x1l_s%d" % s, [128, 2, D], F32) as x1l, \
            nc.sbuf_tensor("ot_s%d" % s, [128, 2, D], F32) as ot, \
            nc.sbuf_tensor("std_s%d" % s, [128, 2, 8], F32) as std:
        Rh2 = [Res("h2t0"), Res("h2t1")]
        Rwgu = [Res("wgu0"), Res("wgu1")]
        RaT = Res("aT")
        Rwd = [Res("wd%d" % i) for i in range(4)]
        Rcv = [Res("cv0"), Res("cv1")]
        Rg2 = Res("g2b")
        Rx1 = [Res("x1l0"), Res("x1l1")]
        Rot = [Res("ot0"), Res("ot1")]
        Rsd = [Res("std0"), Res("std1")]
        P.dma(g2b[:, :], C.gvec_d[s * 2 + 1, :, :].rearrange("a b -> (a b)").partition_broadcast(128), wr=[Rg2])
        n_w = 0
        n_d = 0
        n_o = 0

        def load_h2(ti):
            t0 = starts[ti]
            P.dma(h2t[:, ti % 2, :, :], C.h2T_d[:, :, t0:t0 + 512].rearrange("c p t -> p c t"), wr=[Rh2[ti % 2]])

        def load_wgu(k):
            f = k % NFC
            b = k % 2
            P.dma(wgu[:, b, 0, :, :], C.wb["gate"][:, f * 128:(f + 1) * 128].rearrange("(kc p) n -> p kc n", p=128), wr=[Rwgu[b]])
            P.dma(wgu[:, b, 1, :, :], C.wb["up"][:, f * 128:(f + 1) * 128].rearrange("(kc p) n -> p kc n", p=128), wr=[Rwgu[b]])

        load_h2(0)
        load_wgu(0)
        ntile = len(starts)
        for ti, t0 in enumerate(starts):
            hb = ti % 2
            if ti + 1 < ntile:
                load_h2(ti + 1)
            for f in range(NFC):
                b = n_w % 2
                n_w += 1
                if n_w < ntile * NFC:
                    load_wgu(n_w)
                c = f % 2
                kg, ku = 2 * c, 2 * c + 1
                P.mm(PS[kg][:, 0:512], [(wgu[:, b, 0, kc, :], h2t[:, hb, kc, :]) for kc in range(16)], rd=[Rwgu[b], Rh2[hb]], wr=[PSR[kg]])
                P.mm(PS[ku][:, 0:512], [(wgu[:, b, 1, kc, :], h2t[:, hb, kc, :]) for kc in range(16)], rd=[Rwgu[b], Rh2[hb]], wr=[PSR[ku]])
                cw = sm("convw")
                t1 = cv[:, c, 0, 1:511]
                t2 = cv[:, c, 1, 1:511]
                P.act(t1, PS[kg][:, 1:511], AF.Identity, rd=[PSR[kg], C.Rsmall], wr=[Rcv[c]], scale=cw[:, NFC + f:NFC + f + 1],
                      bias=sm("convb")[:, f:f + 1])
                P.V("dve", "scalar_tensor_tensor", [PSR[kg], C.Rsmall, Rcv[c]], [Rcv[c]], t1, PS[kg][:, 0:510], cw[:, f:f + 1], t1, ALU.mult, ALU.add)
                P.V("dve", "scalar_tensor_tensor", [PSR[kg], C.Rsmall, Rcv[c]], [Rcv[c]], t1, PS[kg][:, 2:512], cw[:, 2 * NFC + f:2 * NFC + f + 1], t1,
                    ALU.mult, ALU.add)
                P.act(t2, t1, AF.Gelu_apprx_tanh, rd=[Rcv[c]], wr=[Rcv[c]])
                P.V("dve", "tensor_tensor", [Rcv[c], PSR[ku]], [RaT], aT[:, f, 1:511], t2, PS[ku][:, 1:511], ALU.mult)
            for jp in range(2):
                subs = []
                for j in (2 * jp, 2 * jp + 1):
                    c0 = 1 + 128 * j
                    sz = min(128, 511 - c0)
                    subs.append((j, c0, sz))
                for f in range(NFC):
                    w = n_d % 4
                    n_d += 1
                    P.dma(wd[:, w, :], C.wb["down"][f * 128:(f + 1) * 128, :], wr=[Rwd[w]])
                    for q, (j, c0, sz) in enumerate(subs):
                        for nb in range(4):
                            bank = q * 4 + nb

                            def fn(bank=bank, f=f, c0=c0, sz=sz, w=w, nb=nb):
                                return nc.tensor.matmul(PS[bank][0:sz, 0:512], aT[:, f, c0:c0 + sz], wd[:, w, nb * 512:(nb + 1) * 512],
                                                        start=(f == 0), stop=(f == NFC - 1))
                            P.add("pe", fn, [RaT, Rwd[w]], [PSR[bank]])
                for q, (j, c0, sz) in enumerate(subs):
                    o = n_o % 2
                    n_o += 1
                    tg = t0 + 128 * j
                    P.dma(x1l[0:sz, o, :], C.x1_d[tg:tg + sz, :], wr=[Rx1[o]])
                    P.V("dve", "memset", [], [Rsd[o]], std[:, o, 0:4], 0.0)
                    for nb in range(4):
                        P.act(ot[0:sz, o, nb * 512:(nb + 1) * 512], PS[q * 4 + nb][0:sz, 0:512], AF.Square, rd=[PSR[q * 4 + nb], Rsd[o]],
                              wr=[Rot[o], Rsd[o]], accum_out=std[0:sz, o, nb:nb + 1])
                    P.V("dve", "tensor_reduce", [Rsd[o]], [Rsd[o]], std[:, o, 4:5], std[:, o, 0:4], AX.X, ALU.add)
                    P.act(std[:, o, 5:6], std[:, o, 4:5], AF.Sqrt, rd=[Rsd[o], C.Rsmall], wr=[Rsd[o]], scale=1.0 / D, bias=sm("eps", 0, 1))
                    P.V("dve", "reciprocal", [Rsd[o]], [Rsd[o]], std[:, o, 6:7], std[:, o, 5:6])
                    for nb in range(4):
                        sl = slice(nb * 512, (nb + 1) * 512)
                        P.V("dve", "scalar_tensor_tensor", [PSR[q * 4 + nb], Rsd[o], Rg2, Rot[o]], [Rot[o]], ot[0:sz, o, sl],
                            PS[q * 4 + nb][0:sz, 0:512], std[0:sz, o, 6:7], g2b[0:sz, sl], ALU.mult, ALU.mult)
                    P.V("pool", "tensor_tensor", [Rot[o], Rx1[o]], [Rot[o]], ot[0:sz, o, :], ot[0:sz, o, :], x1l[0:sz, o, :], ALU.add)
                    P.dma(C.y_out[s][tg:tg + sz, :], ot[0:sz, o, :], rd=[Rot[o]])


def emit_sequence(C, s):
    P = C.P
    for ch in range(28):
        P.dma(C.zT_d[ch, :, 1 + C.seq_T[s]:2 + C.seq_T[s]], C.zt[:, 0:1], rd=[C.Rzt], slow=True)
    phase_A(C, s)
    P.barrier()
    if C.upto < 3:
        return
    phase_attn(C, s)
    if C.do_rwkv:
        phase_rwkv(C, s)
    P.barrier()
    if C.debug:
        d1 = C.nc.dram_tensor("dbg_rw%d" % s, [8, 128, C.TM], BF16, kind="ExternalOutput").ap()
        d2 = C.nc.dram_tensor("dbg_y%d" % s, [8, C.TM // 512, 64, 8, 128], F32, kind="ExternalOutput").ap()
        for cc in range(8):
            P.dma(d1[cc, :, :], C.rwT_d[cc, :, :])
            P.dma(d2[cc, :, :, :, :], C.yscr[cc, :, :, :, :])
        P.barrier()
    if C.upto < 4:
        return
    phase_C(C, s)
    P.barrier()
    if C.upto < 5:
        return
    phase_D(C, s)
    P.barrier()


def phase_rwkv(C, s):
    nc, P = C.nc, C.P
    T = C.seq_T[s]
    sm, upbf = C.sm, C.upbf
    PS, PSR, PB, PBR = C.PS, C.PSR, C.PB, C.PBR
    NBK = T // 512
    LN = 0.60653066
    yscr = C.yscr
    ident = C.ident
    import contextlib
    with contextlib.ExitStack() as _es:
        def _sb(name, shape, dt):
            return _es.enter_context(nc.sbuf_tensor("%s_s%d" % (name, s), shape, dt))
        zin = _sb("zin", [128, 7, 514], F32)
        c0t = _sb("c0t", [128, 28], F32)
        F = _sb("F", [128, 24, 512], F32)
        Lb = _sb("Lb", [128, 4, 512], BF16)
        Hb = _sb("Hb", [128, 3, 512], BF16)
        AR = _sb("AR", [64, 2, 8, 2, 64], BF16)
        BK = _sb("BK", [64, 2, 2, 512], BF16)
        TK = _sb("TK", [64, 3, 8, 128], BF16)
        GC = _sb("GC", [64, 2, 8], F32)
        Asb = _sb("Asb", [64, 16, 320], BF16)
        Mst = _sb("Mst", [64, 16, 64], BF16)
        Mc = _sb("Mc", [64, 2, 8, 64], BF16)
        Pw = _sb("Pw", [64, 2, 8, 128], BF16)
        Sf = _sb("Sf", [64, 2, 64], F32)
        Sb = _sb("Sb", [64, 2, 64], BF16)
        XU = _sb("XU", [64, 2, 128], BF16)
        Yb = _sb("Yb", [64, 3, 8, 128], F32)
        gst = _sb("gst", [64, 64], F32)
        ob = _sb("ob", [128, 512], BF16)
        R = {k: Res(k) for k in ("zin", "c0t", "F", "Lb", "Hb", "AR", "BK", "TK", "GC", "Asb", "Mst", "Mc", "Pw", "Sf", "Sb",
                                 "XU", "Yb", "gst", "ob")}
        mu = sm("mu")
        P.V("dve", "tensor_tensor", [C.Rsmall], [R["c0t"]], c0t[:, :], mu[:, 0:28], mu[:, 28:56], ALU.add)
        P.V("dve", "tensor_scalar", [R["c0t"]], [R["c0t"]], c0t[:, :], c0t[:, :], -1.0, 1.0, ALU.mult, ALU.add)
        Fi = {}

        def Fv(name):
            if name not in Fi:
                Fi[name] = len(Fi)
            return F[:, Fi[name], :]

        def shift(dst, slot, ch, rows=128):
            x = zin[0:rows, slot, :]
            P.act(dst[0:rows], x[:, 1:513], AF.Identity, rd=[R["zin"], R["c0t"]], wr=[R["F"]], scale=c0t[0:rows, ch:ch + 1])
            P.V("dve", "scalar_tensor_tensor", [R["zin"], C.Rsmall, R["F"]], [R["F"]], dst[0:rows], x[:, 0:512], mu[0:rows, ch:ch + 1],
                dst[0:rows], ALU.mult, ALU.add)
            P.V("dve", "scalar_tensor_tensor", [R["zin"], C.Rsmall, R["F"]], [R["F"]], dst[0:rows], x[:, 2:514], mu[0:rows, 28 + ch:29 + ch],
                dst[0:rows], ALU.mult, ALU.add)

        def vv(eng, name, *a):
            P.V(eng, name, [R["F"], C.Rsmall, R["GC"]], [R["F"]], *a)

        def prep(cc, blk, z, final):
            t0 = blk * 512
            for slot, ch in enumerate((cc, 8 + cc, 16 + cc, 24, 25, 26)):
                P.dma(zin[:, slot, :], C.zT_d[ch, :, t0:t0 + 514], wr=[R["zin"]])
            P.dma(zin[0:32, 6, :], C.zT_d[27, 0:32, t0:t0 + 514], wr=[R["zin"]])
            r, k, v, t1, t2 = Fv("r"), Fv("k"), Fv("v"), Fv("t1"), Fv("t2")
            shift(r, 0, cc)
            shift(k, 1, 8 + cc)
            shift(v, 2, 16 + cc)
            shift(t1, 3, 24)
            P.act(Lb[:, 0, :], t1, AF.Tanh, rd=[R["F"]], wr=[R["Lb"]])
            shift(t1, 4, 25)
            P.act(Lb[:, 1, :], t1, AF.Copy, rd=[R["F"]], wr=[R["Lb"]])
            zs = slice(z * 64, z * 64 + 64)
            cs_ = slice(cc * 128, cc * 128 + 128)
            sig, cum, gi, ge, gv, gr = Fv("sig"), Fv("cum"), Fv("gi"), Fv("ge"), Fv("gv"), Fv("gr")
            P.mm(PS[0][:, 0:512], [(upbf[zs, cs_], Lb[zs, 0, :])], rd=[C.Rup, R["Lb"]], wr=[PSR[0]])
            P.act(sig, PS[0][:, 0:512], AF.Sigmoid, rd=[PSR[0], C.Rsmall], wr=[R["F"]], bias=sm("dbase")[:, z * 8 + cc:z * 8 + cc + 1])
            P.add("dve", lambda: nc.vector.tensor_tensor_scan(cum, sm("segmask"), sig, 0.0, ALU.mult, ALU.add), [R["F"], C.Rsmall], [R["F"]])
            c3 = cum.rearrange("p (c t) -> p c t", t=64)
            if z == 1:
                tot = c3[:, :, 63:64].to_broadcast([128, 8, 64])
                vv("dve", "tensor_tensor", t2.rearrange("p (c t) -> p c t", t=64), tot, c3, ALU.subtract)
                vv("dve", "tensor_tensor", cum, t2, sig, ALU.add)
            e_idx = 63 if z == 0 else 0
            tot = c3[:, :, e_idx:e_idx + 1].to_broadcast([128, 8, 64])
            vv("dve", "tensor_tensor", t2.rearrange("p (c t) -> p c t", t=64), tot, c3, ALU.subtract)
            P.act(gr, t2, AF.Exp, rd=[R["F"]], wr=[R["F"]], scale=-LN)
            P.act(gi, cum, AF.Exp, rd=[R["F"]], wr=[R["F"]], scale=-LN)
            P.act(gv, cum, AF.Exp, rd=[R["F"]], wr=[R["F"]], scale=LN)
            vv("dve", "tensor_tensor", t2, cum, sig, ALU.subtract)
            P.act(ge, t2, AF.Exp, rd=[R["F"]], wr=[R["F"]], scale=-LN)
            g3 = gi.rearrange("p (c t) -> p c t", t=64)
            for hs in range(2):
                P.V("dve", "tensor_copy", [R["F"]], [R["GC"]], GC[:, hs, :], g3[hs * 64:hs * 64 + 64, :, e_idx])
            a = Fv("a")
            P.mm(PS[0][:, 0:512], [(upbf[zs, 1024 + cc * 128:1024 + cc * 128 + 128], Lb[zs, 1, :])], rd=[C.Rup, R["Lb"]], wr=[PSR[0]])
            P.act(a, PS[0][:, 0:512], AF.Sigmoid, rd=[PSR[0], C.Rsmall], wr=[R["F"]], bias=sm("ibase")[:, z * 8 + cc:z * 8 + cc + 1])
            kk, kd, bb = Fv("kk"), Fv("kd"), Fv("bb")
            vv("dve", "tensor_scalar", kk, k, sm("kk")[:, cc:cc + 1], None, ALU.mult)
            vv("pool", "tensor_tensor", t1, kk, kk, ALU.mult)
            P.mm(PS[0][:, 0:512], [(sm("bones"), t1)], rd=[C.Rsmall, R["F"]], wr=[PSR[0]])
            P.act(t2, PS[0][:, 0:512], AF.Sqrt, rd=[PSR[0]], wr=[R["F"]])
            vv("dve", "tensor_scalar", t2, t2, 1e-12, None, ALU.max)
            vv("dve", "reciprocal", t2, t2)
            vv("dve", "tensor_tensor", kk, kk, t2, ALU.mult)
            vv("dve", "tensor_scalar", t1, a, -1.0, sm("ka")[:, cc:cc + 1], ALU.add, ALU.mult)
            vv("pool", "tensor_tensor", t1, t1, k, ALU.mult)
            vv("pool", "tensor_tensor", kd, t1, k, ALU.add)
            vv("pool", "tensor_tensor", bb, kk, a, ALU.mult)
            for hs in range(2):
                hp = slice(hs * 64, hs * 64 + 64)
                rdl = [R["F"]]
                P.V("dve", "scalar_tensor_tensor", rdl, [R["AR"]], AR[:, hs, :, 0, :], kk[hp].rearrange("p (c t) -> p c t", t=64), -1.0,
                    ge[hp].rearrange("p (c t) -> p c t", t=64), ALU.mult, ALU.mult)
                P.V("dve", "tensor_tensor", rdl, [R["AR"]], AR[:, hs, :, 1, :], r[hp].rearrange("p (c t) -> p c t", t=64),
                    gi[hp].rearrange("p (c t) -> p c t", t=64), ALU.mult)
                P.V("dve", "tensor_tensor", rdl, [R["BK"]], BK[:, hs, 0, :], bb[hp], gv[hp], ALU.mult)
                P.V("dve", "tensor_tensor", rdl, [R["BK"]], BK[:, hs, 1, :], kd[hp], gv[hp], ALU.mult)
            P.V("pool", "tensor_tensor", [R["F"]], [R["Hb"]], Hb[:, 0, :], bb, gr, ALU.mult)
            P.V("pool", "tensor_tensor", [R["F"]], [R["Hb"]], Hb[:, 1, :], kd, gr, ALU.mult)
            P.V("pool", "tensor_copy", [R["F"]], [R["Hb"]], Hb[:, 2, :], v)
            for w in range(3):
                for c in range(8):
                    P.tr(PB[0][0:64, c * 128:(c + 1) * 128], Hb[:, w, c * 64:(c + 1) * 64], ident, rd=[R["Hb"], C.Rcbf], wr=[PBR[0]])
                P.act(TK[:, w, :, :], PB[0][0:64, 0:1024].rearrange("p (c n) -> p c n", n=128), AF.Copy, rd=[PBR[0]], wr=[R["TK"]])

        def chain_block(cc, blk, z):
            m0 = z * 320
            for c in range(8):
                for hs in range(2):
                    u = c * 2 + hs
                    bk = 1 + hs
                    arv = AR[:, hs, c, :, :].rearrange("p a t -> p (a t)")
                    P.mm(PS[bk][0:64, 0:128], [(BK[:, hs, 0, c * 64:c * 64 + 64], arv)], rd=[R["BK"], R["AR"]], wr=[PSR[bk]])
                    P.mm(PS[bk][0:64, 128:256], [(BK[:, hs, 1, c * 64:c * 64 + 64], arv)], rd=[R["BK"], R["AR"]], wr=[PSR[bk]])
                    P.mm(PS[bk][0:64, 256:320], [(AR[:, hs, c, 0, :], BK[:, hs, 0, c * 64:c * 64 + 64])], rd=[R["BK"], R["AR"]], wr=[PSR[bk]])
                    P.V("dve", "tensor_tensor", [PSR[bk], C.Rcbf], [R["Asb"]], Asb[:, u, :], PS[bk][0:64, 0:320], C.rmask[0:64, m0:m0 + 320], ALU.mult)
            for rd_ in range(2):
                u0 = rd_ * 8
                P.V("pool", "tensor_tensor", [R["Asb"], C.Rcbf], [R["Mc"]], Mc[:, 0, :, :], Asb[:, u0:u0 + 8, 0:64],
                    ident[0:64, 0:64].unsqueeze(1).to_broadcast([64, 8, 64]), ALU.add)
                for lvl in range(5):
                    par = lvl % 2
                    for j in range(8):
                        if lvl == 0:
                            pk, pkt = Asb[:, u0 + j, 0:64], Asb[:, u0 + j, 256:320]
                            rr = [R["Asb"]]
                        else:
                            pk, pkt = Pw[:, 1 - par, j, 0:64], Pw[:, 1 - par, j, 64:128]
                            rr = [R["Pw"]]
                        bk = 3 + j // 4
                        P.mm(PS[bk][0:64, (j % 4) * 128:(j % 4) * 128 + 64], [(pkt, pk)], rd=rr, wr=[PSR[bk]])
                        P.mm(PS[bk][0:64, (j % 4) * 128 + 64:(j % 4) * 128 + 128], [(pk, pkt)], rd=rr, wr=[PSR[bk]])
                    for hb in range(2):
                        P.act(Pw[:, par, hb * 4:hb * 4 + 4, :], PS[3 + hb][0:64, 0:512].rearrange("p (j n) -> p j n", n=128), AF.Copy,
                              rd=[PSR[3 + hb]], wr=[R["Pw"]])
                    for j in range(8):
                        P.mm(PS[5][0:64, j * 64:(j + 1) * 64], [(Pw[:, par, j, 64:128], Mc[:, lvl % 2, j, :])], rd=[R["Pw"], R["Mc"]], wr=[PSR[5]])
                    P.V("dve", "tensor_tensor", [PSR[5], R["Mc"]], [R["Mc"]], Mc[:, (lvl + 1) % 2, :, :], Mc[:, lvl % 2, :, :],
                        PS[5][0:64, 0:512].rearrange("p (j n) -> p j n", n=64), ALU.add)
                P.V("pool", "tensor_copy", [R["Mc"]], [R["Mst"]], Mst[:, u0:u0 + 8, :], Mc[:, 1, :, :])
            order = range(8) if z == 0 else range(7, -1, -1)
            for c in order:
                for hs in range(2):
                    u = c * 2 + hs
                    hsl = slice(hs * 64, hs * 64 + 64)
                    P.mm(PS[1][0:64, hsl], [(AR[:, hs, c, 0, :], Sb[:, hs, :]), (Asb[:, u, 128:192], TK[:, 2, c, hsl])],
                         rd=[R["AR"], R["Sb"], R["Asb"], R["TK"]], wr=[PSR[1]])
                P.act(XU[:, 0, :], PS[1][0:64, 0:128], AF.Copy, rd=[PSR[1]], wr=[R["XU"]])
                for hs in range(2):
                    u = c * 2 + hs
                    hsl = slice(hs * 64, hs * 64 + 64)
                    P.mm(PS[2][0:64, hsl], [(Mst[:, u, :], XU[:, 0, hsl])], rd=[R["Mst"], R["XU"]], wr=[PSR[2]])
                P.V("dve", "tensor_copy", [PSR[2]], [R["XU"]], XU[:, 1, :], PS[2][0:64, 0:128])
                for hs in range(2):
                    u = c * 2 + hs
                    hsl = slice(hs * 64, hs * 64 + 64)
                    P.mm(PS[1][0:64, 256 + hs * 64:256 + hs * 64 + 64],
                         [(AR[:, hs, c, 1, :], Sb[:, hs, :]), (Asb[:, u, 64:128], XU[:, 1, hsl]), (Asb[:, u, 192:256], TK[:, 2, c, hsl])],
                         rd=[R["AR"], R["Sb"], R["Asb"], R["XU"], R["TK"]], wr=[PSR[1]])
                    P.mm(PS[2][0:64, 256 + hs * 64:256 + hs * 64 + 64],
                         [(TK[:, 0, c, hsl], XU[:, 1, hsl]), (TK[:, 1, c, hsl], TK[:, 2, c, hsl])],
                         rd=[R["TK"], R["XU"]], wr=[PSR[2]])
                P.act(Yb[:, 0, c, :], PS[1][0:64, 256:384], AF.Copy, rd=[PSR[1]], wr=[R["Yb"]])
                for hs in range(2):
                    P.V("dve", "scalar_tensor_tensor", [PSR[2], R["GC"], R["Sf"]], [R["Sf"]], Sf[:, hs, :], Sf[:, hs, :], GC[:, hs, c:c + 1],
                        PS[2][0:64, 256 + hs * 64:256 + hs * 64 + 64], ALU.mult, ALU.add)
                P.V("pool", "tensor_copy", [R["Sf"]], [R["Sb"]], Sb[:, :, :], Sf[:, :, :])

        def finalize(cc, blk):
            t0 = blk * 512
            P.dma(Yb[:, 1, :, :], yscr[cc, blk, :, :, :], wr=[R["Yb"]])
            Y = Yb[:, 0, :, :]
            P.V("pool", "tensor_tensor", [R["Yb"]], [R["Yb"]], Y, Y, Yb[:, 1, :, :], ALU.add)
            Y3 = Y.rearrange("p c (h i) -> p (c h) i", i=64)
            W3 = Yb[:, 2, :, :].rearrange("p c (h i) -> p (c h) i", i=64)
            st = gst
            P.V("dve", "tensor_reduce", [R["Yb"]], [R["gst"]], st[:, 0:16], Y3, AX.X, ALU.add)
            P.V("dve", "tensor_scalar", [R["gst"]], [R["gst"]], st[:, 0:16], st[:, 0:16], 1.0 / 64, None, ALU.mult)
            P.V("dve", "tensor_tensor", [R["Yb"], R["gst"]], [R["Yb"]], Y3, Y3, st[:, 0:16].unsqueeze(2).to_broadcast([64, 16, 64]), ALU.subtract)
            P.V("pool", "tensor_tensor", [R["Yb"]], [R["Yb"]], W3, Y3, Y3, ALU.mult)
            P.V("dve", "tensor_reduce", [R["Yb"]], [R["gst"]], st[:, 16:32], W3, AX.X, ALU.add)
            P.act(st[:, 32:48], st[:, 16:32], AF.Sqrt, rd=[R["gst"], C.Rsmall], wr=[R["gst"]], scale=1.0 / 64, bias=sm("eps", 1, 2)[0:64, :])
            P.V("dve", "reciprocal", [R["gst"]], [R["gst"]], st[:, 48:64], st[:, 32:48])
            P.V("dve", "tensor_tensor", [R["Yb"], R["gst"]], [R["Yb"]], Y3, Y3, st[:, 48:64].unsqueeze(2).to_broadcast([64, 16, 64]), ALU.mult)
            for c in range(8):
                P.tr(PS[0][:, c * 64:(c + 1) * 64], Yb[:, 0, c, :], C.identf[0:64, 0:64], rd=[R["Yb"], C.Rsmall], wr=[PSR[0]])
            yn = Fv("yn")
            P.act(yn, PS[0][:, 0:512], AF.Identity, rd=[PSR[0], C.Rsmall], wr=[R["F"]], scale=sm("lnxw")[:, cc:cc + 1], bias=sm("lnxb")[:, cc:cc + 1])
            a, a0, t1, t2, r, k, v = Fv("a"), Fv("a0"), Fv("t1"), Fv("t2"), Fv("r"), Fv("k"), Fv("v")
            P.mm(PS[0][:, 0:512], [(upbf[0:64, 1024 + cc * 128:1024 + cc * 128 + 128], Lb[0:64, 1, :])], rd=[C.Rup, R["Lb"]], wr=[PSR[0]])
            P.act(a0, PS[0][:, 0:512], AF.Sigmoid, rd=[PSR[0], C.Rsmall], wr=[R["F"]], bias=sm("ibase")[:, cc:cc + 1])
            vv("dve", "tensor_tensor", t1, a, a0, ALU.add)
            vv("dve", "tensor_scalar", t1, t1, -2.0, sm("ka")[:, cc:cc + 1], ALU.add, ALU.mult)
            vv("dve", "tensor_scalar", t1, t1, 2.0, sm("rk")[:, cc:cc + 1], ALU.add, ALU.mult)
            vv("pool", "tensor_tensor", t1, t1, k, ALU.mult)
            vv("pool", "tensor_tensor", t1, t1, r, ALU.mult)
            P.mm(PS[0][:, 0:512], [(sm("bones"), t1)], rd=[C.Rsmall, R["F"]], wr=[PSR[0]])
            P.V("dve", "tensor_tensor", [R["F"], PSR[0]], [R["F"]], t2, PS[0][:, 0:512], v, ALU.mult)
            vv("pool", "tensor_tensor", yn, yn, t2, ALU.add)
            shift(t1, 5, 26)
            P.act(Lb[:, 2, :], t1, AF.Sigmoid, rd=[R["F"]], wr=[R["Lb"]])
            shift(t1, 6, 27, rows=32)
            P.act(Lb[0:32, 3, :], t1[0:32], AF.Sigmoid, rd=[R["F"]], wr=[R["Lb"]])
            P.mm(PS[0][:, 0:512], [(upbf[:, 2048 + cc * 128:2048 + cc * 128 + 128], Lb[:, 2, :]),
                                   (upbf[0:32, 3072 + cc * 128:3072 + cc * 128 + 128], Lb[0:32, 3, :])], rd=[C.Rup, R["Lb"]], wr=[PSR[0]])
            P.V("dve", "tensor_tensor", [R["F"], PSR[0]], [R["ob"]], ob[:, :], yn, PS[0][:, 0:512], ALU.mult)
            P.dma(C.rwT_d[cc, :, t0:t0 + 512], ob[:, :], rd=[R["ob"]])

        for cc in range(8):
            for z in range(2):
                P.V("pool", "memset", [], [R["Sf"]], Sf[:, :, :], 0.0)
                P.V("pool", "memset", [], [R["Sb"]], Sb[:, :, :], 0.0)
                for bi in range(NBK):
                    blk = bi if z == 0 else NBK - 1 - bi
                    prep(cc, blk, z, z == 1)
                    chain_block(cc, blk, z)
                    if z == 0:
                        P.dma(yscr[cc, blk, :, :, :], Yb[:, 0, :, :], rd=[R["Yb"]], wr=[R["ob"]])
                    else:
                        finalize(cc, blk)
            P.barrier()


def _pack_small(inp, TM):
    SO, NSM = _small_layout(TM)
    sm = np.zeros((128, NSM), np.float32)

    def put(key, arr):
        o, n = SO[key]
        assert arr.shape == (128, n), (key, arr.shape, n)
        sm[:, o:o + n] = arr

    def fm(v, nch):
        return np.ascontiguousarray(np.asarray(v, np.float32).reshape(nch, 128).T)

    put("ln", np.concatenate([fm(inp[k][0], 16) for k in ("ln_mix_pre", "ln_mix_post", "ln_ffn_pre", "ln_ffn_post")], 1))
    put("bada", fm(inp["b_ada"][0], 96))
    mu = np.zeros((2, 28 * 128), np.float32)
    mu[:, :RWC] = inp["shift_mu"][0]
    put("mu", np.concatenate([fm(mu[0], 28), fm(mu[1], 28)], 1))
    put("dbase", np.concatenate([fm(inp["decay_base"][0][z], 8) for z in range(2)], 1))
    put("ibase", np.concatenate([fm(inp["iclr_base"][0][z], 8) for z in range(2)], 1))
    put("kk", fm(inp["k_k"][0], 8))
    put("ka", fm(inp["k_a"][0], 8))
    put("rk", fm(inp["r_k"][0].reshape(-1), 8))
    put("lnxw", fm(inp["lnx_w"][0], 8))
    put("lnxb", fm(inp["lnx_b"][0], 8))
    cw = inp["ffn_conv_w"][0]
    put("convw", np.concatenate([fm(cw[k], NFC) for k in range(3)], 1))
    put("convb", fm(inp["ffn_conv_b"][0], NFC))
    p = np.arange(128)
    put("ident", np.eye(128, dtype=np.float32))
    put("swap", (p[:, None] == (p[None, :] + 64) % 128).astype(np.float32))
    put("ones", np.ones((128, 128), np.float32))
    put("bones", (p[:, None] // 64 == p[None, :] // 64).astype(np.float32))
    mk0 = (p[:, None] >= p[None, :]).astype(np.float32)
    mk1 = (p[:, None] <= p[None, :]).astype(np.float32)
    mk0f = mk0 * (p[:, None] >= 64)
    mk1l = mk1 * (p[:, None] < 64)
    put("amask", np.concatenate([mk0, mk1, mk0f, mk1, mk0, mk1l, mk0f, mk1l], 1).astype(np.float32))
    s64 = np.arange(64)
    su = (s64[:, None] < s64[None, :]).astype(np.float32)
    iu = (s64[:, None] <= s64[None, :]).astype(np.float32)
    m0 = np.concatenate([su, iu, su, iu, su.T], 1)
    m1 = np.concatenate([su.T, iu.T, su.T, iu.T, su], 1)
    put("rmask", np.concatenate([np.concatenate([m0, m0], 0), np.concatenate([m1, m1], 0)], 1))
    seg = np.ones((128, 512), np.float32)
    seg[:, ::64] = 0.0
    put("segmask", seg)
    e = np.zeros((128, 2), np.float32)
    e[:, 0] = 1e-6
    e[:, 1] = 64e-5
    put("eps", e)
    return sm


def _cossin(TM):
    half = 64
    inv_freq = (1.0 / (np.float32(10000.0) ** (np.arange(half, dtype=np.float32) / np.float32(half)))).astype(np.float32)
    ang = np.arange(TM, dtype=np.float32)[None, :] * inv_freq[:, None]
    c = np.cos(ang).astype(np.float32)
    sn = np.sin(ang).astype(np.float32)
    return np.concatenate([np.concatenate([c, c], 0), np.concatenate([-sn, sn], 0)], 1)


def make_in_map(inp, xs, cs, TM):
    m = {}
    for i, x in enumerate(xs):
        m["x%d" % i] = np.ascontiguousarray(x, np.float32)
    NS = len(xs)
    cT = np.zeros((128, 16 * NS), np.float32)
    for i, c in enumerate(cs):
        cT[:, i::NS] = np.asarray(c, np.float32).reshape(16, 128).T
    m["cT"] = cT
    return m


_SHARED = {}


def _shared_inputs(inp, TM):
    sh = {}
    sh["small"] = _pack_small(inp, TM)
    sh["cossin"] = _cossin(TM)
    gu = np.zeros((256, 1024), np.float32)
    gu[:160] = inp["gate_up"][0]
    sh["ups"] = np.ascontiguousarray(np.concatenate([np.asarray(inp["decay_up"][0], np.float32).reshape(128, 1024),
                                np.asarray(inp["iclr_up"][0], np.float32).reshape(128, 1024), gu[:128], gu[128:]], 1))
    sh["lnxrow"] = np.concatenate([np.asarray(inp["lnx_w"][0], np.float32).reshape(8, 128),
                                   np.asarray(inp["lnx_b"][0], np.float32).reshape(8, 128)], 0)
    for k, src in (("w_ada", "w_ada"), ("w_in", "w_in"), ("w_att", "w_att_branch"), ("w_rw", "w_rwkv_branch"),
                   ("w_out", "w_out"), ("w_gate", "w_ffn_gate"), ("w_up", "w_ffn_up"), ("w_down", "w_ffn_down")):
        sh[k] = np.ascontiguousarray(np.asarray(inp[src][0], np.float32))
    return sh


def kernel(**inputs):
    inp = {k: np.asarray(v) for k, v in inputs.items()}
    TM = max(SEQ_T)
    nc = build(SEQ_T)
    sh = _shared_inputs(inp, TM)
    in_maps = []
    for i in range(8):
        m = make_in_map(inp, [inp["x_sample"][i], inp["x_prompt"][i // 2]], [inp["c_sample"][i], inp["c_prompt"][i // 2]], TM)
        m.update(sh)
        in_maps.append(m)
    res = run_bass_kernel_spmd(nc, in_maps, core_ids=list(range(8)))
    y_sample = np.stack([np.asarray(res.results[i]["y0"], np.float32) for i in range(8)], 0)
    y_prompt = np.stack([np.asarray(res.results[2 * j]["y1"], np.float32) for j in range(4)], 0)
    return (y_prompt, y_sample)
```
